# Optimizing a Trainium2 kernel written in Bass

```python
import math
import jax, jax.numpy as jnp
from jax import lax
import numpy as np

D_MODEL = 1024
BATCH = 2
SEQ = 8192
DEPTH = 4

A_HEADS = 4
A_QK_DIM = 64
A_V_DIM = 2 * A_QK_DIM
B_Q_HEADS = 8
B_KV_HEADS = 2
B_HEAD_DIM = 64
WINDOW = 128
BAND_BLOCK = 128
C_HEADS = 4
C_Q_RANK = 384
C_KV_RANK = 256
C_NOPE_DIM = 128
C_ROPE_DIM = 64
C_V_DIM = 128
ROPE_THETA = 10000.0
Q_BLOCK = 128
MIX_WIDTH = 512
N_BRANCH = 3
D_FF = 2816
EPS = 1e-6
NEG_INF = -1e30

A_QK_COLS = A_HEADS * 2 * A_QK_DIM
A_V_COLS = A_HEADS * A_V_DIM
B_Q_COLS = B_Q_HEADS * B_HEAD_DIM
B_KV_COLS = B_KV_HEADS * B_HEAD_DIM
IN_SPLIT_SIZES = (A_QK_COLS, A_QK_COLS, A_V_COLS, B_Q_COLS, B_KV_COLS, B_KV_COLS, C_Q_RANK, C_KV_RANK, C_ROPE_DIM)
IN_COLS = sum(IN_SPLIT_SIZES)

kernel_name = "hybrid_diff_window_mla_macaron"


def rms_norm(x, g):
    x32 = x.astype(jnp.float32)
    y = x32 * lax.rsqrt(jnp.mean(x32 * x32, axis=-1, keepdims=True) + EPS) * g.astype(jnp.float32)
    return y.astype(x.dtype)


def swiglu(h, w13, w2):
    a, g = jnp.split(h @ w13, 2, axis=-1)
    return (jax.nn.silu(a) * g) @ w2


def alibi_slopes(n):
    return 2.0 ** (-8.0 * jnp.arange(1, n + 1, dtype=jnp.float32) / n)


def rope(x, cos, sin):
    x1, x2 = jnp.split(x, 2, axis=-1)
    c = cos[None, :, None, :].astype(x.dtype)
    s = sin[None, :, None, :].astype(x.dtype)
    return jnp.concatenate([x1 * c - x2 * s, x1 * s + x2 * c], axis=-1)


def _split_cols(z):
    outs, off = [], 0
    for size in IN_SPLIT_SIZES:
        outs.append(z[..., off:off + size])
        off += size
    return outs


def _query_blocks(t):
    b, s = t.shape[:2]
    return jnp.moveaxis(t.reshape((b, s // Q_BLOCK, Q_BLOCK) + t.shape[2:]), 1, 0)


def _from_blocks(t):
    t = jnp.moveaxis(t, 0, 1)
    return t.reshape((t.shape[0], t.shape[1] * t.shape[2]) + t.shape[3:])


def diff_attention(q, k, v, lam, lam_init, subln_g, slopes):
    b, s_len = q.shape[:2]
    scale = A_QK_DIM ** -0.5
    key_pos = jnp.arange(s_len, dtype=jnp.float32)

    def block(args):
        qb, start = args
        sc = jnp.einsum('bqhmd,bkhmd->bhmqk', qb, k).astype(jnp.float32) * scale
        qpos = start + jnp.arange(Q_BLOCK, dtype=jnp.float32)
        dist = jnp.abs(qpos[:, None] - key_pos[None, :])
        sc = sc - slopes[None, :, None, None, None] * dist
        p = jax.nn.softmax(sc, axis=-1).astype(v.dtype)
        o = jnp.einsum('bhmqk,bkhe->bqhme', p, v)
        return (o[..., 0, :] - lam * o[..., 1, :]).astype(v.dtype)

    starts = jnp.arange(s_len // Q_BLOCK, dtype=jnp.float32) * Q_BLOCK
    o = _from_blocks(lax.map(block, (_query_blocks(q), starts)))
    o = rms_norm(o, subln_g) * (1.0 - lam_init)
    return o.reshape(b, s_len, A_HEADS * A_V_DIM)


def window_gqa(q, k, v, sinks, slopes):
    b, s_len = q.shape[:2]
    nb = s_len // BAND_BLOCK
    g = B_Q_HEADS // B_KV_HEADS
    qb = q.reshape(b, nb, BAND_BLOCK, B_KV_HEADS, g, B_HEAD_DIM)

    def band(t):
        tp = jnp.pad(t, ((0, 0), (BAND_BLOCK, BAND_BLOCK), (0, 0), (0, 0)))
        tp = tp.reshape(b, nb + 2, BAND_BLOCK, B_KV_HEADS, B_HEAD_DIM)
        return jnp.concatenate([tp[:, :-2], tp[:, 1:-1], tp[:, 2:]], axis=2)

    kb, vb = band(k), band(v)
    sc = jnp.einsum('bnqhgd,bnkhd->bnhgqk', qb, kb).astype(jnp.float32) * (B_HEAD_DIM ** -0.5)
    qi = jnp.arange(BAND_BLOCK)
    kj = jnp.arange(3 * BAND_BLOCK) - BAND_BLOCK
    rel = kj[None, :] - qi[:, None]
    kpos = jnp.arange(nb)[:, None] * BAND_BLOCK + kj[None, :]
    valid = (jnp.abs(rel) <= WINDOW)[None] & ((kpos >= 0) & (kpos < s_len))[:, None, :]
    bias = -slopes.reshape(B_KV_HEADS, g)[:, :, None, None] * jnp.abs(rel).astype(jnp.float32)
    sc = jnp.where(valid[None, :, None, None], sc + bias, NEG_INF)
    sink = jnp.broadcast_to(sinks.astype(jnp.float32).reshape(B_KV_HEADS, g)[None, None, :, :, None, None], sc.shape[:-1] + (1,))
    p = jax.nn.softmax(jnp.concatenate([sc, sink], axis=-1), axis=-1)[..., :-1].astype(v.dtype)
    o = jnp.einsum('bnhgqk,bnkhd->bnqhgd', p, vb)
    return o.reshape(b, s_len, B_Q_HEADS * B_HEAD_DIM)


def mla(c_q, c_kv, k_rope, q_norm_g, w_uq, kv_norm_g, w_ukv, cos, sin):
    b, s_len = c_q.shape[:2]
    q = jnp.einsum('bsr,rhe->bshe', rms_norm(c_q, q_norm_g), w_uq)
    q_nope, q_pe = q[..., :C_NOPE_DIM], rope(q[..., C_NOPE_DIM:], cos, sin)
    kv = jnp.einsum('bsr,rhe->bshe', rms_norm(c_kv, kv_norm_g), w_ukv)
    k_nope, v = kv[..., :C_NOPE_DIM], kv[..., C_NOPE_DIM:]
    k_pe = rope(k_rope[:, :, None, :], cos, sin)[:, :, 0, :]
    scale = (C_NOPE_DIM + C_ROPE_DIM) ** -0.5

    def block(args):
        qn, qp = args
        sc = (jnp.einsum('bqhd,bkhd->bhqk', qn, k_nope) + jnp.einsum('bqhd,bkd->bhqk', qp, k_pe)).astype(jnp.float32) * scale
        p = jax.nn.softmax(sc, axis=-1).astype(v.dtype)
        return jnp.einsum('bhqk,bkhd->bqhd', p, v)

    o = _from_blocks(lax.map(block, (_query_blocks(q_nope), _query_blocks(q_pe))))
    return o.reshape(b, s_len, C_HEADS * C_V_DIM)


def setup_inputs(seed: int = 0) -> dict:
    key = jax.random.key(seed)
    ks = jax.random.split(key, 24)

    def nrm(k, shape, scale):
        return jax.random.normal(k, shape, jnp.float32) * scale

    def gain(k, shape):
        return 1.0 + 0.05 * jax.random.normal(k, shape, jnp.float32)

    L, D, F = DEPTH, D_MODEL, D_FF
    return {
        "x": nrm(ks[0], (BATCH, SEQ, D), 1.0),
        "ffn1_norm": gain(ks[1], (L, D)),
        "ffn1_w13": nrm(ks[2], (L, D, 2 * F), D ** -0.5),
        "ffn1_w2": nrm(ks[3], (L, F, D), F ** -0.5),
        "mix_norm": gain(ks[4], (L, D)),
        "w_in": nrm(ks[5], (L, D, IN_COLS), D ** -0.5),
        "w_gate": nrm(ks[6], (L, D, N_BRANCH * D), D ** -0.5),
        "b_gate": nrm(ks[7], (L, N_BRANCH * D), 0.01),
        "a_lambda": nrm(ks[8], (L, 4, A_QK_DIM), 0.1),
        "a_subln": gain(ks[9], (L, A_V_DIM)),
        "b_sinks": nrm(ks[10], (L, B_Q_HEADS), 0.5),
        "c_q_norm": gain(ks[11], (L, C_Q_RANK)),
        "c_w_uq": nrm(ks[12], (L, C_Q_RANK, C_HEADS, C_NOPE_DIM + C_ROPE_DIM), C_Q_RANK ** -0.5),
        "c_kv_norm": gain(ks[13], (L, C_KV_RANK)),
        "c_w_ukv": nrm(ks[14], (L, C_KV_RANK, C_HEADS, C_NOPE_DIM + C_V_DIM), C_KV_RANK ** -0.5),
        "w_branch": nrm(ks[15], (L, N_BRANCH, MIX_WIDTH, D), MIX_WIDTH ** -0.5),
        "w_out": nrm(ks[16], (L, D, D), D ** -0.5),
        "ffn2_norm": gain(ks[17], (L, D)),
        "ffn2_w13": nrm(ks[18], (L, D, 2 * F), D ** -0.5),
        "ffn2_w2": nrm(ks[19], (L, F, D), F ** -0.5),
        "final_norm": gain(ks[20], (D,)),
    }


def reference(x, ffn1_norm, ffn1_w13, ffn1_w2, mix_norm, w_in, w_gate, b_gate, a_lambda, a_subln, b_sinks, c_q_norm, c_w_uq, c_kv_norm, c_w_ukv, w_branch, w_out, ffn2_norm, ffn2_w13, ffn2_w2, final_norm):
    b, s_len, _ = x.shape
    pos = jnp.arange(s_len, dtype=jnp.float32)
    inv_freq = ROPE_THETA ** (-jnp.arange(0, C_ROPE_DIM, 2, dtype=jnp.float32) / C_ROPE_DIM)
    ang = pos[:, None] * inv_freq[None, :]
    cos, sin = jnp.cos(ang), jnp.sin(ang)
    slopes_a = alibi_slopes(A_HEADS)
    slopes_b = alibi_slopes(B_Q_HEADS)

    for l in range(DEPTH):
        x = x + 0.5 * swiglu(rms_norm(x, ffn1_norm[l]), ffn1_w13[l], ffn1_w2[l])

        h = rms_norm(x, mix_norm[l])
        aq, ak, av, bq, bk, bv, cq, ckv, ckr = _split_cols(h @ w_in[l])

        lam_init = 0.8 - 0.6 * math.exp(-0.3 * l)
        lp = a_lambda[l].astype(jnp.float32)
        lam = jnp.exp(jnp.sum(lp[0] * lp[1])) - jnp.exp(jnp.sum(lp[2] * lp[3])) + lam_init
        oa = diff_attention(aq.reshape(b, s_len, A_HEADS, 2, A_QK_DIM),
                            ak.reshape(b, s_len, A_HEADS, 2, A_QK_DIM),
                            av.reshape(b, s_len, A_HEADS, A_V_DIM),
                            lam, lam_init, a_subln[l], slopes_a)
        ob = window_gqa(bq.reshape(b, s_len, B_Q_HEADS, B_HEAD_DIM),
                        bk.reshape(b, s_len, B_KV_HEADS, B_HEAD_DIM),
                        bv.reshape(b, s_len, B_KV_HEADS, B_HEAD_DIM),
                        b_sinks[l], slopes_b)
        oc = mla(cq, ckv, ckr, c_q_norm[l], c_w_uq[l], c_kv_norm[l], c_w_ukv[l], cos, sin)

        branches = jnp.stack([oa, ob, oc], axis=2)
        proj = jnp.einsum('bsnw,nwd->bsnd', branches, w_branch[l])
        gates = jax.nn.sigmoid(h @ w_gate[l] + b_gate[l]).reshape(b, s_len, N_BRANCH, D_MODEL)
        x = x + jnp.sum(gates * proj, axis=2) @ w_out[l]

        x = x + 0.5 * swiglu(rms_norm(x, ffn2_norm[l]), ffn2_w13[l], ffn2_w2[l])

    return rms_norm(x, final_norm)
```

```python
import math
import contextlib
import numpy as np
import ml_dtypes
import concourse.bass as bass
import concourse.mybir as mybir
from concourse.bass_utils import run_bass_kernel_spmd

F32, BF16 = mybir.dt.float32, mybir.dt.bfloat16
AF = mybir.ActivationFunctionType
ALU = mybir.AluOpType
AX = mybir.AxisListType

NCORES = 8
D = 1024
SEQ = 8192
T = 2048
NT = 4
KC = 8
DEPTH = 4
FF = 2816
NMP = 22
EPS = 1e-6

AK, AV, BK, BV, CKN, CV, CKPE, PACK_ROWS = 0, 512, 1024, 1152, 1280, 1792, 2304, 2368
CH = 256
NCHK = (PACK_ROWS + CH - 1) // CH
CHK_ORDER = [4, 0, 1, 2, 3, 5, 6, 7, 8, 9]
QA_, QB_, QCN_, QCPE_, Q_ROWS = 0, 512, 1024, 1536, 1792
OA_, OB_, OC_, O_ROWS = 0, 512, 1024, 1536
C_GF1, C_GMIX, C_GF2, C_GQ, C_GKV, C_BG, C_SUBLN, C_SINK, C_LAMI, C_OML, C_GFIN, C_ALAM, C_EPS, C_HB, C_NFLAG, NCONST = (
    0, 8, 16, 24, 27, 29, 53, 54, 58, 59, 60, 68, 324, 325, 333, 349)

SLOPES_A = [2.0 ** (-8.0 * (i + 1) / 4) for i in range(4)]
SLOPES_B = [2.0 ** (-8.0 * (i + 1) / 8) for i in range(8)]


class S:
    __slots__ = ("h",)

    def __init__(self, h):
        self.h = h


class Buf:
    __slots__ = ("ap", "w", "r")

    def __init__(self, ap):
        self.ap = ap
        self.w = {}
        self.r = {}

    def __getitem__(self, idx):
        return V([self], self.ap[idx])


class V:
    __slots__ = ("bs", "ap")

    def __init__(self, bs, ap):
        self.bs = bs
        self.ap = ap

    def __getitem__(self, idx):
        return V(self.bs, self.ap[idx])


class Q:
    def __init__(self, cx, eng, name, strict=False, ndma=0):
        self.eng, self.name, self.strict = eng, name, strict
        self.sem = cx.sem(name)
        self.cnt = 0
        self.seen = {}
        self.dsl = [[cx.sem(f"{name}d{i}"), 0] for i in range(ndma)]
        self.di = 0


class Rot:
    def __init__(self, items):
        self.items, self.i = items, 0

    def next(self):
        b = self.items[self.i % len(self.items)]
        self.i += 1
        return b


class Cx:
    def __init__(self, nc):
        self.nc = nc
        self.es = contextlib.ExitStack()
        self.stk = [self.es]
        self.uid = 0
        self.pe = Q(self, nc.tensor, "pe")
        self.act = Q(self, nc.scalar, "act", strict=True)
        self.dve = Q(self, nc.vector, "dve", strict=True)
        self.pool = Q(self, nc.gpsimd, "pool", strict=True, ndma=24)
        self.sp = Q(self, nc.sync, "sp", ndma=32)
        self.qs = [self.pe, self.act, self.dve, self.pool, self.sp]

    def sem(self, name):
        return S(self.es.enter_context(self.nc.semaphore(name)))

    def sbt(self, name, shape, dt):
        self.uid += 1
        return self.stk[-1].enter_context(self.nc.sbuf_tensor(f"{name}_{self.uid}", shape, dt))

    def sb(self, name, shape, dt):
        return Buf(self.sbt(name, shape, dt))

    def pst(self, name):
        return Buf(self.es.enter_context(self.nc.psum_tensor(name, [128, 512], F32)))

    def dram(self, name, shape, dt, kind):
        return Buf(self.nc.dram_tensor(name, list(shape), dt, kind=kind).ap())

    @contextlib.contextmanager
    def scope(self):
        st = contextlib.ExitStack()
        self.stk.append(st)
        try:
            yield
        finally:
            self.barrier()
            self.stk.pop()
            st.close()

    def _wait(self, q, need):
        for s, v in need.items():
            if s is q.sem and not q.strict:
                continue
            if q.seen.get(s, 0) >= v:
                continue
            q.eng.wait_ge(s.h, v)
            q.seen[s] = v

    def _deps(self, q, reads, writes):
        need = {}
        for b in reads:
            for s, v in b.w.items():
                if need.get(s, 0) < v:
                    need[s] = v
        for b in writes:
            for dct in (b.w, b.r):
                for s, v in dct.items():
                    if need.get(s, 0) < v:
                        need[s] = v
        self._wait(q, need)

    @staticmethod
    def _commit(s, v, reads, writes):
        for b in reads:
            if b.r.get(s, 0) < v:
                b.r[s] = v
        for b in writes:
            if b.w.get(s, 0) < v:
                b.w[s] = v

    def op(self, q, fn, reads, writes):
        self._deps(q, reads, writes)
        q.cnt += 1
        fn().then_inc(q.sem.h, 1)
        self._commit(q.sem, q.cnt, reads, writes)

    def dma(self, q, out, in_, **kw):
        self._deps(q, in_.bs, out.bs)
        sl = q.dsl[q.di % len(q.dsl)]
        q.di += 1
        if sl[1] and q.seen.get(sl[0], 0) < sl[1]:
            q.eng.wait_ge(sl[0].h, sl[1])
            q.seen[sl[0]] = sl[1]
        sl[1] += 16
        q.eng.dma_start(out=out.ap, in_=in_.ap, **kw).then_inc(sl[0].h, 16)
        self._commit(sl[0], sl[1], in_.bs, out.bs)

    def barrier(self):
        need = {}
        for q in self.qs:
            if q.cnt:
                need[q.sem] = q.cnt
            for sl in q.dsl:
                if sl[1]:
                    need[sl[0]] = sl[1]
        for q in self.qs:
            self._wait(q, need)

    def mm(self, out, lhsT, rhs, start=True, stop=True, **kw):
        nc = self.nc
        self.op(self.pe, lambda: nc.tensor.matmul(out.ap, lhsT.ap, rhs.ap, start=start, stop=stop, **kw),
                lhsT.bs + rhs.bs, out.bs)

    def actf(self, out, in_, func, bias=None, scale=None):
        nc = self.nc
        reads = list(in_.bs)
        kw = {}
        for nm, val in (("bias", bias), ("scale", scale)):
            if val is None:
                continue
            if isinstance(val, V):
                reads += val.bs
                kw[nm] = val.ap
            else:
                kw[nm] = val
        self.op(self.act, lambda: nc.scalar.activation(out=out.ap, in_=in_.ap, func=func, **kw), reads, out.bs)

    def tt(self, out, a, b, op, q=None):
        q = q or self.dve
        self.op(q, lambda: q.eng.tensor_tensor(out=out.ap, in0=a.ap, in1=b.ap, op=op), a.bs + b.bs, out.bs)

    def stt(self, out, a, scalar, b, op0, op1):
        q = self.dve
        reads = a.bs + b.bs
        sc = scalar
        if isinstance(scalar, V):
            reads = reads + scalar.bs
            sc = scalar.ap
        self.op(q, lambda: q.eng.scalar_tensor_tensor(out=out.ap, in0=a.ap, scalar=sc, in1=b.ap, op0=op0, op1=op1),
                reads, out.bs)

    def ts(self, out, a, s1, s2, op0, op1=None, q=None):
        q = q or self.dve
        reads = list(a.bs)
        v1, v2 = s1, s2
        if isinstance(s1, V):
            reads += s1.bs
            v1 = s1.ap
        if isinstance(s2, V):
            reads += s2.bs
            v2 = s2.ap
        kw = {} if op1 is None else {"op1": op1}
        self.op(q, lambda: q.eng.tensor_scalar(out=out.ap, in0=a.ap, scalar1=v1, scalar2=v2, op0=op0, **kw),
                reads, out.bs)

    def recip(self, out, a):
        q = self.dve
        self.op(q, lambda: q.eng.reciprocal(out=out.ap, in_=a.ap), a.bs, out.bs)

    def copy(self, out, a, q=None):
        q = q or self.dve
        if q is self.act:
            self.op(q, lambda: q.eng.activation(out=out.ap, in_=a.ap, func=AF.Copy), a.bs, out.bs)
        else:
            self.op(q, lambda: q.eng.tensor_copy(out=out.ap, in_=a.ap), a.bs, out.bs)

    def memset(self, out, val, q=None):
        q = q or self.dve
        self.op(q, lambda: q.eng.memset(out.ap, val), [], out.bs)

    def rsum(self, out, a):
        q = self.dve
        self.op(q, lambda: q.eng.reduce_sum(out=out.ap, in_=a.ap, axis=AX.X), a.bs, out.bs)

    def finish(self):
        need = {}
        for q in self.qs:
            if q.cnt:
                need[q.sem] = q.cnt
            for sl in q.dsl:
                if sl[1]:
                    need[sl[0]] = sl[1]
        self._wait(self.sp, need)
        self.es.close()


def pkv(dr, r0, n):
    c, o = divmod(r0, CH)
    return dr["pack"][c][o:o + n, :]


def pav(dr, j, r0, n):
    c, o = divmod(r0, CH)
    return dr["packall"][c][j, o:o + n, :]


def rr(v, pat, **kw):
    return V(v.bs, v.ap.rearrange(pat, **kw))


class WStream:
    def __init__(self, cx, bufs, srcs):
        self.cx, self.bufs, self.srcs, self.issued = cx, bufs, srcs, 0

    def ahead(self, i):
        lim = min(len(self.srcs), i + len(self.bufs))
        while self.issued < lim:
            j = self.issued
            b = self.bufs[j % len(self.bufs)]
            self.cx.dma(self.cx.pool, b[:], self.srcs[j])
            self.issued += 1

    def get(self, i):
        self.ahead(i)
        return self.bufs[i % len(self.bufs)]


class G:
    pass


def setup_globals(cx):
    g = G()
    g.P = [cx.pst(f"ps{i}") for i in range(8)]
    g.ones = cx.sb("ones", [128, 128], BF16)
    cx.memset(g.ones[:], 1.0)
    g.onesLR = [cx.sb("onesL", [128, 128], BF16), cx.sb("onesR", [128, 128], BF16)]
    for e in range(2):
        cx.memset(g.onesLR[e][:], 0.0)
        cx.memset(g.onesLR[e][:, e * 64:(e + 1) * 64], 1.0)
    g.C = cx.sb("consts", [128, NCONST], F32)
    g.eps = g.C[:, C_EPS:C_EPS + 1]
    return g


def rms_norm(cx, g, xs, gain, outs, dfeat, ps, tmpf, sqp):
    nk = len(xs)
    for k in range(nk):
        sq = sqp.next()
        cx.actf(sq[:], xs[k], AF.Square)
        cx.mm(ps[:], g.ones[:], sq[:], start=(k == 0), stop=(k == nk - 1))
    sd = tmpf.next()
    cx.actf(sd[:], ps[:], AF.Sqrt, bias=g.eps, scale=1.0 / dfeat)
    rs = tmpf.next()
    cx.recip(rs[:], sd[:])
    for k in range(nk):
        gk = V(gain.bs, gain.ap[:, k:k + 1])
        cx.stt(outs[k], xs[k], gk, rs[:], ALU.mult, ALU.mult)


def alloc_xT(cx):
    t = cx.sbt("xT", [128, KC, T], F32)
    return t, [[Buf(t[:, kc, tc * 512:(tc + 1) * 512]) for tc in range(NT)] for kc in range(KC)]


def load_xT(cx, xt, xT, src):
    for kc in range(KC):
        cx.dma(cx.sp, V(xT[kc], xt[:, kc, :]), src[kc * 128:(kc + 1) * 128, :])


def store_xT(cx, xt, xT, dst):
    for kc in range(KC):
        cx.dma(cx.sp, dst[kc * 128:(kc + 1) * 128, :], V(xT[kc], xt[:, kc, :]))


def ffn(cx, g, xT, gain, w13d, w2d):
    FT = 2
    P = g.P
    with cx.scope():
        hT = [[cx.sb("h", [128, 512], BF16) for s in range(FT)] for kc in range(KC)]
        hid = [[cx.sb("hid", [128, 512], BF16) for s in range(FT)] for mp in range(NMP)]
        tmpf = Rot([cx.sb("tf", [128, 512], F32) for _ in range(4)])
        sqp = Rot([cx.sb("sq", [128, 512], BF16) for _ in range(3)])
        ng = NT // FT
        s13 = WStream(cx, [cx.sb("w13", [128, KC, 256], BF16) for _ in range(3)],
                      [w13d[mp] for _ in range(ng) for mp in range(NMP)])
        s2 = WStream(cx, [cx.sb("w2", [128, NMP, 128], BF16) for _ in range(3)],
                     [w2d[n] for _ in range(ng) for n in range(KC)])
        pa_rot = Rot([P[1], P[2]])
        pg_rot = Rot([P[3], P[4]])
        py_rot = Rot([P[5], P[6]])
        s13.ahead(0)
        for gi in range(ng):
            for s in range(FT):
                tc = gi * FT + s
                rms_norm(cx, g, [xT[kc][tc][:] for kc in range(KC)], gain,
                         [hT[kc][s][:] for kc in range(KC)], D, P[0], tmpf, sqp)
            s2.ahead(gi * KC)
            for mp in range(NMP):
                wt = s13.get(gi * NMP + mp)
                for s in range(FT):
                    pa, pg = pa_rot.next(), pg_rot.next()
                    for kc in range(KC):
                        cx.mm(pa[:], wt[:, kc, 0:128], hT[kc][s][:], start=(kc == 0), stop=(kc == KC - 1))
                    for kc in range(KC):
                        cx.mm(pg[:], wt[:, kc, 128:256], hT[kc][s][:], start=(kc == 0), stop=(kc == KC - 1))
                    sa = tmpf.next()
                    cx.actf(sa[:], pa[:], AF.Silu)
                    cx.tt(hid[mp][s][:], sa[:], pg[:], ALU.mult)
            if gi + 1 < ng:
                s13.ahead((gi + 1) * NMP)
            for n in range(KC):
                wt = s2.get(gi * KC + n)
                for s in range(FT):
                    tc = gi * FT + s
                    py = py_rot.next()
                    for mp in range(NMP):
                        cx.mm(py[:], wt[:, mp, :], hid[mp][s][:], start=(mp == 0), stop=(mp == NMP - 1))
                    cx.stt(xT[n][tc][:], py[:], 0.5, xT[n][tc][:], ALU.mult, ALU.add)


def rope(cx, p1, p2, cc, ss, out, tmpf):
    t1, t2 = tmpf.next(), tmpf.next()
    cx.tt(t1[0:64, :], p1, cc[0:64, :], ALU.mult)
    cx.tt(t2[0:64, :], p2, ss[0:64, :], ALU.mult)
    cx.tt(out, t1[0:64, :], t2[0:64, :], ALU.add)


def mixer_proj(cx, g, xT, dr):
    P = g.P
    C = g.C
    qbuf = dr["qbuf"]
    pmm = Rot([P[1], P[2], P[3], P[4], P[5], P[6]])
    ev = Rot([cx.act, cx.dve])
    with cx.scope():
        hT = [[cx.sb("h", [128, 512], BF16) for tc in range(NT)] for kc in range(KC)]
        tmpf = Rot([cx.sb("tf", [128, 512], F32) for _ in range(6)])
        sqp = Rot([cx.sb("sq", [128, 512], BF16) for _ in range(3)])
        for tc in range(NT):
            rms_norm(cx, g, [xT[kc][tc][:] for kc in range(KC)], C[:, C_GMIX:C_GMIX + 8],
                     [hT[kc][tc][:] for kc in range(KC)], D, P[0], tmpf, sqp)
        wc = [cx.sb("wc", [128, KC, 128], BF16) for _ in range(6)]
        wuq = cx.sb("wuq", [128, 3, 4, 256], BF16)
        wukv = cx.sb("wukv", [128, 2, 1024], BF16)
        for i in range(6):
            cx.dma(cx.pool, wc[i][:], dr["winF"][13 + i])
        cx.dma(cx.pool, wuq[:], dr["wuq"][:])
        cx.dma(cx.pool, wukv[:], dr["wukv"][:])
        with cx.scope():
            ropet = Rot([cx.sb("rope", [64, 2, 512], F32) for _ in range(4)])
            cst = Rot([cx.sb("cst", [128, 512], BF16) for _ in range(8)])
            cvst = Rot([cx.sb("cvst", [128, 4, 512], BF16) for _ in range(2)])
            cqf = [cx.sb("cqf", [128, 512], F32) for _ in range(5)]
            cqn = [cx.sb("cqn", [128, 512], BF16) for _ in range(5)]
            for tc in range(NT):
                tsl = slice(tc * 512, (tc + 1) * 512)
                rt = ropet.next()
                cx.dma(cx.sp, rt[:, 0, :], dr["rope"][0, :, tsl])
                cx.dma(cx.sp, rt[:, 1, :], dr["rope"][1, :, tsl])
                cc, ss = rt[:, 0, :], rt[:, 1, :]
                for i in range(5):
                    p = pmm.next()
                    for kc in range(KC):
                        cx.mm(p[:], wc[i][:, kc, :], hT[kc][tc][:], start=(kc == 0), stop=(kc == KC - 1))
                    cx.copy(cqf[i][:], p[:], q=ev.next())
                p1, p2 = pmm.next(), pmm.next()
                for kc in range(KC):
                    cx.mm(p1[0:64, :], wc[5][:, kc, 0:64], hT[kc][tc][:], start=(kc == 0), stop=(kc == KC - 1))
                for kc in range(KC):
                    cx.mm(p2[0:64, :], wc[5][:, kc, 64:128], hT[kc][tc][:], start=(kc == 0), stop=(kc == KC - 1))
                st = cst.next()
                rope(cx, p1[0:64, :], p2[0:64, :], cc, ss, st[0:64, :], tmpf)
                cx.dma(cx.sp, pkv(dr, CKPE, 64)[:, tsl], st[0:64, :])
                rms_norm(cx, g, [cqf[i][:] for i in range(3)], C[:, C_GQ:C_GQ + 3], [cqn[i][:] for i in range(3)],
                         384, P[0], tmpf, sqp)
                rms_norm(cx, g, [cqf[3 + i][:] for i in range(2)], C[:, C_GKV:C_GKV + 2], [cqn[3 + i][:] for i in range(2)],
                         256, P[0], tmpf, sqp)
                for h in range(4):
                    p = pmm.next()
                    for kc in range(3):
                        cx.mm(p[:], wuq[:, kc, h, 0:128], cqn[kc][:], start=(kc == 0), stop=(kc == 2))
                    st = cst.next()
                    cx.copy(st[:], p[:], q=ev.next())
                    cx.dma(cx.sp, qbuf[QCN_ + h * 128:QCN_ + (h + 1) * 128, tsl], st[:])
                    p1, p2 = pmm.next(), pmm.next()
                    for kc in range(3):
                        cx.mm(p1[0:64, :], wuq[:, kc, h, 128:192], cqn[kc][:], start=(kc == 0), stop=(kc == 2))
                    for kc in range(3):
                        cx.mm(p2[0:64, :], wuq[:, kc, h, 192:256], cqn[kc][:], start=(kc == 0), stop=(kc == 2))
                    st = cst.next()
                    rope(cx, p1[0:64, :], p2[0:64, :], cc, ss, st[0:64, :], tmpf)
                    cx.dma(cx.sp, qbuf[QCPE_ + h * 64:QCPE_ + (h + 1) * 64, tsl], st[0:64, :])
                    p = pmm.next()
                    for kc in range(2):
                        cx.mm(p[:], wukv[:, kc, h * 128:(h + 1) * 128], cqn[3 + kc][:], start=(kc == 0), stop=(kc == 1))
                    st = cst.next()
                    cx.copy(st[:], p[:], q=ev.next())
                    cx.dma(cx.sp, pkv(dr, CKN + h * 128, 128)[:, tsl], st[:])
                cv = cvst.next()
                for blk in range(4):
                    p = pmm.next()
                    for kc in range(2):
                        cx.mm(p[:, 0:512], cqn[3 + kc][:, blk * 128:(blk + 1) * 128], wukv[:, kc, 512:1024],
                              start=(kc == 0), stop=(kc == 1))
                    cx.copy(cv[:, blk, :], p[:], q=ev.next())
                for h in range(4):
                    cx.dma(cx.sp, rr(pkv(dr, CV + h * 128, 128)[:, tsl], "p (b f) -> p b f", f=128),
                           cv[:, :, h * 128:(h + 1) * 128])
        with cx.scope():
            order = [12, 4, 5, 6, 7, 0, 1, 2, 3, 8, 9, 10, 11]
            wT = cx.sb("wT", [128, KC, 640], BF16)
            cx.dma(cx.pool, wT[:], dr["winT"][:])
            wf = WStream(cx, [cx.sb("wf", [128, KC, 128], BF16) for _ in range(3)], [dr["winF"][c] for c in order])
            stage = Rot([cx.sb("st", [128, T], BF16) for _ in range(2)])
            dests = ([qbuf[QA_ + c * 128:QA_ + (c + 1) * 128, :] for c in range(4)]
                     + [pkv(dr, AK + c * 128, 128) for c in range(4)]
                     + [qbuf[QB_ + c * 128:QB_ + (c + 1) * 128, :] for c in range(4)]
                     + [pkv(dr, BK, 128)])
            vst = Rot([cx.sb("vst", [128, 4, 512], BF16) for _ in range(2)])
            vsb = Rot([cx.sb("vsb", [128, 4, 128], BF16) for _ in range(2)])
            wf.ahead(0)

            def fm_chunk(i):
                c = order[i]
                wt = wf.get(i)
                st = stage.next()
                for tc in range(NT):
                    p = pmm.next()
                    for kc in range(KC):
                        cx.mm(p[:], wt[:, kc, :], hT[kc][tc][:], start=(kc == 0), stop=(kc == KC - 1))
                    cx.copy(st[:, tc * 512:(tc + 1) * 512], p[:], q=ev.next())
                cx.dma(cx.sp, dests[c], st[:])

            for tc in range(NT):
                vs = vsb.next()
                for blk in range(4):
                    pB = pmm.next()
                    for kc in range(KC):
                        cx.mm(pB[:, 0:128], hT[kc][tc][:, blk * 128:(blk + 1) * 128], wT[:, kc, 512:640],
                              start=(kc == 0), stop=(kc == KC - 1))
                    cx.copy(vs[:, blk, :], pB[:, 0:128], q=ev.next())
                cx.dma(cx.sp, rr(pkv(dr, BV, 128)[:, tc * 512:(tc + 1) * 512], "p (b f) -> p b f", f=128), vs[:])
            fm_chunk(0)
            allgather(cx, dr, 4)
            for i in range(1, 5):
                fm_chunk(i)
            for tc in range(NT):
                vs = vst.next()
                for blk in range(4):
                    pA = pmm.next()
                    for kc in range(KC):
                        cx.mm(pA[:, 0:512], hT[kc][tc][:, blk * 128:(blk + 1) * 128], wT[:, kc, 0:512],
                              start=(kc == 0), stop=(kc == KC - 1))
                    cx.copy(vs[:, blk, :], pA[:, 0:512], q=ev.next())
                for h in range(4):
                    cx.dma(cx.sp,
                           rr(pkv(dr, AV + h * 128, 128)[:, tc * 512:(tc + 1) * 512], "p (b f) -> p b f", f=128),
                           vs[:, :, h * 128:(h + 1) * 128])
            for i in range(5, 13):
                fm_chunk(i)


def att_b(cx, g, dr):
    P, C = g.P, g.C
    qbuf, obuf = dr["qbuf"], dr["obuf"]
    NKI = 24
    with cx.scope():
        KB = [cx.sb("kb", [128, NKI * 128], BF16) for _ in range(2)]
        VBt = cx.sb("vbt", [128, NKI, 128], BF16)
        for gk in range(2):
            r0 = BK + gk * 64
            for s_ in range(4):
                cx.dma(cx.sp, KB[gk][0:64, s_ * 128:(s_ + 1) * 128], pav(dr, s_, r0, 64)[:, 15 * 128:16 * 128])
                cx.dma(cx.sp, KB[gk][0:64, (20 + s_) * 128:(21 + s_) * 128], pav(dr, s_, r0, 64)[:, 0:128])
            cx.dma(cx.sp, KB[gk][0:64, 512:512 + T], pkv(dr, r0, 64))
        for s_ in range(4):
            cx.dma(cx.sp, VBt[:, s_, :], pav(dr, s_, BV, 128)[:, 15 * 128:16 * 128])
            cx.dma(cx.sp, VBt[:, 20 + s_, :], pav(dr, s_, BV, 128)[:, 0:128])
        cx.dma(cx.sp, VBt[:, 4:20, :], rr(pkv(dr, BV, 128), "p (b f) -> p b f", f=128))
        VLR = [[cx.sb("vlr", [128, NKI, 128], BF16) for e in range(2)] for gk in range(2)]
        for gk in range(2):
            for e in range(2):
                cx.memset(VLR[gk][e][:], 0.0)
                cx.copy(VLR[gk][e][:, :, e * 64:(e + 1) * 64], VBt[:, :, gk * 64:(gk + 1) * 64])
        BH = cx.sb("bh", [128, 8, 384], F32)
        cx.dma(cx.sp, BH[:], dr["bh"][:])
        esk = cx.sb("esk", [128, 4], F32)
        cx.actf(esk[:], C[:, C_SINK:C_SINK + 4], AF.Exp)
        qall = [[cx.sb("qbt", [128, T], BF16) for e in range(2)] for c in range(4)]
        for c in range(4):
            for e in range(2):
                r0 = QB_ + (2 * c + e) * 64
                cx.dma(cx.sp, qall[c][e][0:64, :], qbuf[r0:r0 + 64, :])
        cx._wait(cx.pool, {sl[0]: sl[1] for sl in cx.sp.dsl if sl[1]})
        for i in (0, 1, 2, 3):
            allgather(cx, dr, i)
        ost = Rot([cx.sb("ost", [128, T], BF16) for _ in range(2)])
        tmpf = Rot([cx.sb("tf", [128, 512], F32) for _ in range(8)])
        ptp = Rot([cx.sb("pt", [128, 512], BF16) for _ in range(8)])
        pO = Rot([P[3], P[4]])
        pS = Rot([P[5], P[6]])
        steps = []
        for c in range(4):
            for Gq in range(4):
                items = []
                for kp in range(4 * Gq - 1, 4 * Gq + 5):
                    if kp == -1:
                        items += [(s_, kp, C_HB + s_) for s_ in range(4)]
                    elif kp == 16:
                        items += [(20 + s_, kp, C_HB + 4 + s_) for s_ in range(4)]
                    else:
                        items.append((4 + kp, kp, None))
                for ii, (ki, kp, bcol) in enumerate(items):
                    for e in range(2):
                        steps.append(dict(c=c, Gq=Gq, ki=ki, kp=kp, bcol=bcol, e=e, first=(ii == 0 and e == 0),
                                          last=(ii == len(items) - 1 and e == 1)))
        state = {}
        LOOK = 3
        psc = Rot([P[0], P[1], P[2], P[7]])

        def emit_s(st):
            c, Gq, kp, e = st["c"], st["Gq"], st["kp"], st["e"]
            gk = c // 2
            QBt = qall[c]
            n_lo, n_hi = max(4 * Gq, kp - 1), min(4 * Gq + 3, kp + 1)
            nq = (n_hi - n_lo + 1) * 128
            c_lo = n_lo - kp + 1
            st["nq"], st["col0"] = nq, (n_lo - 4 * Gq) * 128
            pc = psc.next()
            cx.mm(pc[:, 0:nq], KB[gk][0:64, st["ki"] * 128:(st["ki"] + 1) * 128], QBt[e][0:64, n_lo * 128:(n_hi + 1) * 128])
            sp_ = tmpf.next()
            cx.tt(sp_[:, 0:nq], pc[:, 0:nq], BH[:, 2 * c + e, c_lo * 128:c_lo * 128 + nq], ALU.add)
            pt = ptp.next()
            bias = None if st["bcol"] is None else C[:, st["bcol"]:st["bcol"] + 1]
            cx.actf(pt[:, 0:nq], sp_[:, 0:nq], AF.Exp, scale=0.125, bias=bias)
            st["pt"] = pt

        def emit_pv(st):
            c, Gq, e = st["c"], st["Gq"], st["e"]
            gk = c // 2
            if st["first"]:
                state[("po", c, Gq)] = (pO.next(), pS.next())
                if Gq == 0:
                    state[("os", c)] = ost.next()
            po, ps_ = state[("po", c, Gq)]
            nq, col0, pt = st["nq"], st["col0"], st["pt"]
            cx.mm(po[:, col0:col0 + nq], VLR[gk][e][:, st["ki"], :], pt[:, 0:nq], start=st["first"], stop=st["last"],
                  skip_group_check=True)
            cx.mm(ps_[:, col0:col0 + nq], g.onesLR[e][:], pt[:, 0:nq], start=st["first"], stop=st["last"],
                  skip_group_check=True)
            if st["last"]:
                os_ = state[("os", c)]
                den = tmpf.next()
                cx.ts(den[:], ps_[:], esk[:, c:c + 1], None, ALU.add)
                rc = tmpf.next()
                cx.recip(rc[:], den[:])
                cx.tt(os_[:, Gq * 512:(Gq + 1) * 512], po[:], rc[:], ALU.mult)
                if Gq == 3:
                    cx.dma(cx.sp, obuf[OB_ + c * 128:OB_ + (c + 1) * 128, :], os_[:])

        for idx in range(len(steps) + LOOK):
            if idx < len(steps):
                emit_s(steps[idx])
            if idx - LOOK >= 0:
                emit_pv(steps[idx - LOOK])


def flash(nkb, s_fn, pv_fn, look=2):
    for kb in range(nkb + look):
        if kb < nkb:
            s_fn(kb)
        if kb - look >= 0:
            pv_fn(kb - look)


def att_ac(cx, g, dr):
    P, C = g.P, g.C
    qbuf, obuf = dr["qbuf"], dr["obuf"]
    with cx.scope():
        class KS:
            pass
        sets = []
        for i in range(2):
            k = KS()
            k.KT = [[cx.sb("kt", [128, T], BF16) for j in range(4)] for m in range(2)]
            k.V = [cx.sb("vv", [128, T], BF16) for j in range(4)]
            k.Q = [cx.sb("qq", [128, T], BF16) for m in range(2)]
            sets.append(k)
        KPE = [cx.sb("kpe", [128, T], BF16) for j in range(4)]
        Dw = cx.sb("dw", [128, 896], F32)
        cx.dma(cx.sp, Dw[:], dr["dw"][:])
        ost = Rot([cx.sb("ost", [128, T], BF16) for _ in range(2)])
        tmpf = Rot([cx.sb("tf", [128, 512], F32) for _ in range(6)])
        tkeep = [cx.sb("tk", [128, 512], F32) for _ in range(2)]
        sqp = Rot([cx.sb("sq", [128, 512], BF16) for _ in range(2)])
        ptp = Rot([cx.sb("pt", [128, 512], BF16) for _ in range(6)])
        psc = Rot([P[0], P[1], P[2], P[7]])
        pOs = [P[3], P[4]]
        pSs = [P[5], P[6]]
        sm = cx.sb("sm", [128, 16], F32)
        pr = cx.sb("pr", [128, 128], F32)
        cx.tt(pr[:, 0:64], C[:, C_ALAM:C_ALAM + 64], C[:, C_ALAM + 64:C_ALAM + 128], ALU.mult)
        cx.tt(pr[:, 64:128], C[:, C_ALAM + 128:C_ALAM + 192], C[:, C_ALAM + 192:C_ALAM + 256], ALU.mult)
        cx.rsum(sm[:, 0:1], pr[:, 0:64])
        cx.rsum(sm[:, 1:2], pr[:, 64:128])
        cx.actf(sm[:, 2:4], sm[:, 0:2], AF.Exp)
        cx.tt(sm[:, 4:5], sm[:, 3:4], sm[:, 2:3], ALU.subtract)
        cx.tt(sm[:, 5:6], sm[:, 4:5], C[:, C_LAMI:C_LAMI + 1], ALU.subtract)
        cx.tt(sm[:, 6:7], C[:, C_SUBLN:C_SUBLN + 1], C[:, C_OML:C_OML + 1], ALU.mult)
        neglam, gsub = sm[:, 5:6], sm[:, 6:7]

        def loads(step):
            k = sets[step % 2]
            if step < 4:
                h = step
                for m in range(2):
                    cx.dma(cx.sp, k.Q[m][64:96, :], dr["qaug"][:, :])
                    r0 = QA_ + h * 128 + m * 64
                    cx.dma(cx.sp, k.Q[m][0:64, :], qbuf[r0:r0 + 64, :])
                for j in range(4):
                    for m in range(2):
                        r0 = AK + h * 128 + m * 64
                        cx.dma(cx.sp, k.KT[m][j][0:64, :], pav(dr, j, r0, 64))
                        cx.dma(cx.sp, k.KT[m][j][64:96, :], dr["kaug"][h, :, j * T:(j + 1) * T])
                    cx.dma(cx.sp, k.V[j][:], pav(dr, j, AV + h * 128, 128))
            else:
                h = step - 4
                if h == 0:
                    for j in range(4):
                        cx.dma(cx.sp, KPE[j][0:64, :], pav(dr, j, CKPE, 64))
                cx.dma(cx.sp, k.Q[0][:], qbuf[QCN_ + h * 128:QCN_ + (h + 1) * 128, :])
                cx.dma(cx.sp, k.Q[1][0:64, :], qbuf[QCPE_ + h * 64:QCPE_ + (h + 1) * 64, :])
                for j in range(4):
                    cx.dma(cx.sp, k.KT[0][j][:], pav(dr, j, CKN + h * 128, 128))
                    cx.dma(cx.sp, k.V[j][:], pav(dr, j, CV + h * 128, 128))

        def run_a(step):
            k = sets[step % 2]
            h = step
            os_ = ost.next()
            for qc in range(NT):
                qs = slice(qc * 512, (qc + 1) * 512)
                for m in range(2):
                    po, ps_ = pOs[m], pSs[m]
                    pcs, pts = {}, {}

                    def s_fn(kb, m=m):
                        j, blk = kb // 16, kb % 16
                        pc = psc.next()
                        cx.mm(pc[:], k.KT[m][j][0:96, blk * 128:(blk + 1) * 128], k.Q[m][0:96, qs])
                        pt = ptp.next()
                        if 4 * qc <= blk < 4 * qc + 4:
                            off = (blk - 4 * qc) * 128
                            sp_ = tmpf.next()
                            fl = C[:, C_NFLAG + h * 4 + j:C_NFLAG + h * 4 + j + 1]
                            cx.stt(sp_[:], Dw[:, 384 - off:384 - off + 512], fl, pc[:], ALU.mult, ALU.add)
                            cx.actf(pt[:], sp_[:], AF.Exp, scale=0.125)
                        else:
                            cx.actf(pt[:], pc[:], AF.Exp, scale=0.125)
                        pts[kb] = pt

                    def pv_fn(kb):
                        j, blk = kb // 16, kb % 16
                        pt = pts.pop(kb)
                        cx.mm(po[:], k.V[j][:, blk * 128:(blk + 1) * 128], pt[:], start=(kb == 0), stop=(kb == 63))
                        cx.mm(ps_[:], g.ones[:], pt[:], start=(kb == 0), stop=(kb == 63))

                    flash(64, s_fn, pv_fn, look=3)
                    r = tmpf.next()
                    cx.recip(r[:], ps_[:])
                    cx.tt(tkeep[m][:], po[:], r[:], ALU.mult)
                o = tmpf.next()
                cx.stt(o[:], tkeep[1][:], neglam, tkeep[0][:], ALU.mult, ALU.add)
                sq = sqp.next()
                cx.actf(sq[:], o[:], AF.Square)
                pSS = psc.next()
                cx.mm(pSS[:], g.ones[:], sq[:])
                sd = tmpf.next()
                cx.actf(sd[:], pSS[:], AF.Sqrt, bias=g.eps, scale=1.0 / 128)
                rs = tmpf.next()
                cx.recip(rs[:], sd[:])
                cx.stt(os_[:, qs], o[:], gsub, rs[:], ALU.mult, ALU.mult)
            cx.dma(cx.sp, obuf[OA_ + h * 128:OA_ + (h + 1) * 128, :], os_[:])

        def run_c(step):
            k = sets[step % 2]
            h = step - 4
            sc = 192.0 ** -0.5
            os_ = ost.next()
            for qc in range(NT):
                qs = slice(qc * 512, (qc + 1) * 512)
                po, ps_ = pOs[qc % 2], pSs[qc % 2]
                pts = {}

                def s_fn(kb):
                    j, blk = kb // 16, kb % 16
                    ks = slice(blk * 128, (blk + 1) * 128)
                    pc = psc.next()
                    cx.mm(pc[:], k.KT[0][j][:, ks], k.Q[0][:, qs], start=True, stop=False)
                    cx.mm(pc[:], KPE[j][0:64, ks], k.Q[1][0:64, qs], start=False, stop=True)
                    pt = ptp.next()
                    cx.actf(pt[:], pc[:], AF.Exp, scale=sc)
                    pts[kb] = pt

                def pv_fn(kb):
                    j, blk = kb // 16, kb % 16
                    pt = pts.pop(kb)
                    cx.mm(po[:], k.V[j][:, blk * 128:(blk + 1) * 128], pt[:], start=(kb == 0), stop=(kb == 63))
                    cx.mm(ps_[:], g.ones[:], pt[:], start=(kb == 0), stop=(kb == 63))

                flash(64, s_fn, pv_fn, look=3)
                r = tmpf.next()
                cx.recip(r[:], ps_[:])
                cx.tt(os_[:, qs], po[:], r[:], ALU.mult)
            cx.dma(cx.sp, obuf[OC_ + h * 128:OC_ + (h + 1) * 128, :], os_[:])

        loads(0)
        cx._wait(cx.pool, {sl[0]: sl[1] for sl in cx.sp.dsl if sl[1]})
        for i in (5, 6, 7, 8, 9):
            allgather(cx, dr, i)
        for step in range(8):
            if step + 1 < 8:
                loads(step + 1)
            if step < 4:
                run_a(step)
            else:
                run_c(step)


def merge(cx, g, xT, dr):
    P, C = g.P, g.C
    obuf = dr["obuf"]
    with cx.scope():
        wout = cx.sb("wout", [128, KC, D], BF16)
        cx.dma(cx.pool, wout[:, 0:4, :], dr["wout"][:, 0:4, :])
        cx.dma(cx.pool, wout[:, 4:8, :], dr["wout"][:, 4:8, :])
        sg = WStream(cx, [cx.sb("wg", [128, KC, 3, 128], BF16) for _ in range(3)],
                     [dr["wg"][n] for _ in range(NT) for n in range(KC)])
        sbr = WStream(cx, [cx.sb("wb", [128, 3, 4, 128], BF16) for _ in range(3)],
                      [dr["wb"][n] for _ in range(NT) for n in range(KC)])
        hT = [cx.sb("h", [128, 512], BF16) for kc in range(KC)]
        mg = [cx.sb("mg", [128, 512], BF16) for kc in range(KC)]
        otp = Rot([cx.sb("ot", [128, 12, 512], BF16) for _ in range(2)])
        tmpf = Rot([cx.sb("tf", [128, 512], F32) for _ in range(6)])
        accs = Rot([cx.sb("acc", [128, 512], F32) for _ in range(2)])
        sqp = Rot([cx.sb("sq", [128, 512], BF16) for _ in range(3)])
        pmm = Rot([P[1], P[2], P[3], P[4], P[5], P[6]])
        sg.ahead(0)
        sbr.ahead(0)
        for tc in range(NT):
            tsl = slice(tc * 512, (tc + 1) * 512)
            ot = otp.next()
            cx.dma(cx.sp, ot[:], V([obuf], obuf.ap[:, tsl].rearrange("(i p) t -> p i t", p=128)))
            rms_norm(cx, g, [xT[kc][tc][:] for kc in range(KC)], C[:, C_GMIX:C_GMIX + 8],
                     [hT[kc][:] for kc in range(KC)], D, P[0], tmpf, sqp)
            for n in range(KC):
                wgt = sg.get(tc * KC + n)
                wbt = sbr.get(tc * KC + n)
                acc = accs.next()
                for br in range(3):
                    pg = pmm.next()
                    for kc in range(KC):
                        cx.mm(pg[:], wgt[:, kc, br, :], hT[kc][:], start=(kc == 0), stop=(kc == KC - 1))
                    gt = tmpf.next()
                    cx.actf(gt[:], pg[:], AF.Sigmoid, bias=C[:, C_BG + br * 8 + n:C_BG + br * 8 + n + 1])
                    pp = pmm.next()
                    for wcn in range(4):
                        cx.mm(pp[:], wbt[:, br, wcn, :], ot[:, br * 4 + wcn, :], start=(wcn == 0), stop=(wcn == 3))
                    if br == 0:
                        cx.tt(acc[:], gt[:], pp[:], ALU.mult)
                    else:
                        t2 = tmpf.next()
                        cx.tt(t2[:], gt[:], pp[:], ALU.mult)
                        if br == 1:
                            cx.tt(acc[:], acc[:], t2[:], ALU.add)
                        else:
                            cx.tt(mg[n][:], acc[:], t2[:], ALU.add)
            for n2 in range(KC):
                py = pmm.next()
                for n in range(KC):
                    cx.mm(py[:], wout[:, n, n2 * 128:(n2 + 1) * 128], mg[n][:], start=(n == 0), stop=(n == KC - 1))
                cx.tt(xT[n2][tc][:], xT[n2][tc][:], py[:], ALU.add)


def final_norm(cx, g, xT, ydst):
    P, C = g.P, g.C
    with cx.scope():
        tmpf = Rot([cx.sb("tf", [128, 512], F32) for _ in range(4)])
        sqp = Rot([cx.sb("sq", [128, 512], BF16) for _ in range(3)])
        yo = Rot([cx.sb("yo", [128, KC, 512], F32) for _ in range(2)])
        for tc in range(NT):
            y = yo.next()
            rms_norm(cx, g, [xT[kc][tc][:] for kc in range(KC)], C[:, C_GFIN:C_GFIN + 8],
                     [y[:, kc, :] for kc in range(KC)], D, P[0], tmpf, sqp)
            cx.dma(cx.sp, V([ydst], ydst.ap[:, tc * 512:(tc + 1) * 512].rearrange("(k p) t -> p k t", p=128)), y[:])


def load_consts(cx, g, src):
    cx.dma(cx.sp, g.C[:], src)


def allgather(cx, dr, i):
    q = cx.pool
    st = dr["cc"][i]
    in_buf, out_buf, out_flat = dr["pack"][i], dr["packall"][i], dr["packall_flat"][i]
    cx._deps(q, [in_buf], [out_buf])
    st[1] += 1
    cx.nc.gpsimd.collective_compute("AllGather", ALU.bypass, replica_groups=[[0, 1, 2, 3], [4, 5, 6, 7]],
                                    ins=[in_buf.ap.opt()], outs=[out_flat.opt()]).then_inc(st[0].h, 1)
    cx._commit(st[0], st[1], [in_buf], [out_buf])


def build_fused(depth=DEPTH):
    nc = bass.Bass("TRN2", target_bir_lowering=False)
    cx = Cx(nc)
    EI = "ExternalInput"
    xin = cx.dram("xT_in", [D, T], F32, EI)
    cd = cx.dram("consts", [DEPTH, 128, NCONST], F32, EI)
    w13a = cx.dram("w13a", [DEPTH, NMP, 128, KC, 256], F32, EI)
    w2a = cx.dram("w2a", [DEPTH, KC, 128, NMP, 128], F32, EI)
    w13b = cx.dram("w13b", [DEPTH, NMP, 128, KC, 256], F32, EI)
    w2b = cx.dram("w2b", [DEPTH, KC, 128, NMP, 128], F32, EI)
    winF = cx.dram("winF", [DEPTH, 19, 128, KC, 128], F32, EI)
    winT = cx.dram("winT", [DEPTH, 128, KC, 640], F32, EI)
    wuq = cx.dram("wuq", [DEPTH, 128, 3, 4, 256], F32, EI)
    wukv = cx.dram("wukv", [DEPTH, 128, 2, 1024], F32, EI)
    wg = cx.dram("wg", [DEPTH, KC, 128, KC, 3, 128], F32, EI)
    wb = cx.dram("wb", [DEPTH, KC, 128, 3, 4, 128], F32, EI)
    wout = cx.dram("wout", [DEPTH, 128, KC, D], F32, EI)
    base = {
        "rope": cx.dram("rope", [2, 64, T], F32, EI),
        "kaug": cx.dram("kaug", [4, 32, SEQ], BF16, EI),
        "qaug": cx.dram("qaug", [32, T], BF16, EI),
        "bh": cx.dram("bh", [128, 8, 384], F32, EI),
        "dw": cx.dram("dw", [128, 896], F32, EI),
    }
    yout = cx.dram("yT", [D, T], F32, "ExternalOutput")
    xs = Buf(nc.dram_tensor("xspill", [D, T], F32).ap())
    crow = [min(CH, PACK_ROWS - i * CH) for i in range(NCHK)]
    pack = [Buf(nc.dram_tensor(f"pack{i}", [crow[i], T], BF16).ap()) for i in range(NCHK)]
    pkflat = [[nc.dram_tensor(f"pkall{p}_{i}", [4 * crow[i], T], BF16).ap() for i in range(NCHK)] for p in range(2)]
    pkall = [[Buf(a.rearrange("(s r) t -> s r t", s=4)) for a in pkflat[p]] for p in range(2)]
    qbuf = Buf(nc.dram_tensor("qbuf", [Q_ROWS, T], BF16).ap())
    obuf = Buf(nc.dram_tensor("obuf", [O_ROWS, T], BF16).ap())
    ccst = [[cx.sem(f"cc{i}"), 0] for i in range(NCHK)]
    g = setup_globals(cx)

    def drl(l):
        d = dict(base)
        d.update({"winF": winF[l], "winT": winT[l], "wuq": wuq[l], "wukv": wukv[l], "wg": wg[l], "wb": wb[l],
                  "wout": wout[l], "pack": pack, "qbuf": qbuf, "obuf": obuf, "cc": ccst})
        d["packall"] = pkall[l % 2]
        d["packall_flat"] = pkflat[l % 2]
        return d

    with cx.scope():
        xt, xT = alloc_xT(cx)
        load_xT(cx, xt, xT, xin)
        load_consts(cx, g, cd[0])
        ffn(cx, g, xT, g.C[:, C_GF1:C_GF1 + 8], w13a[0], w2a[0])
        mixer_proj(cx, g, xT, drl(0))
        store_xT(cx, xt, xT, xs)
    for l in range(depth):
        d = drl(l)
        att_b(cx, g, d)
        att_ac(cx, g, d)
        with cx.scope():
            xt, xT = alloc_xT(cx)
            load_xT(cx, xt, xT, xs)
            merge(cx, g, xT, d)
            ffn(cx, g, xT, g.C[:, C_GF2:C_GF2 + 8], w13b[l], w2b[l])
            if l + 1 < depth:
                load_consts(cx, g, cd[l + 1])
                ffn(cx, g, xT, g.C[:, C_GF1:C_GF1 + 8], w13a[l + 1], w2a[l + 1])
                mixer_proj(cx, g, xT, drl(l + 1))
                store_xT(cx, xt, xT, xs)
            else:
                final_norm(cx, g, xT, yout)
    cx.finish()
    return nc

def _cm(w, cols):
    return np.ascontiguousarray(w[:, cols].reshape(KC, 128, len(cols)).transpose(1, 0, 2))


def prep_w13(w13):
    a = w13[:, :FF].reshape(KC, 128, NMP, 128)
    gg = w13[:, FF:].reshape(KC, 128, NMP, 128)
    return np.ascontiguousarray(np.concatenate([a, gg], axis=3).transpose(2, 1, 0, 3))


def prep_w2(w2):
    return np.ascontiguousarray(w2.reshape(NMP, 128, KC, 128).transpose(2, 1, 0, 3))


def prep_win(w_in):
    chunks = []
    for c in range(4):
        chunks.append(np.arange(c * 128, (c + 1) * 128))
    for c in range(4):
        chunks.append(512 + np.arange(c * 128, (c + 1) * 128))
    for c in range(4):
        chunks.append(1536 + np.arange(c * 128, (c + 1) * 128))
    chunks.append(2048 + np.arange(128))
    for c in range(3):
        chunks.append(2304 + np.arange(c * 128, (c + 1) * 128))
    for c in range(2):
        chunks.append(2688 + np.arange(c * 128, (c + 1) * 128))
    chunks.append(2944 + np.concatenate([np.arange(64), np.arange(32, 64), np.arange(0, 32)]))
    winF = np.stack([_cm(w_in, c) for c in chunks])
    winT = _cm(w_in, np.concatenate([1024 + np.arange(512), 2176 + np.arange(128)]))
    return winF, winT


def prep_wuq(w):
    idx = np.concatenate([np.arange(192), 128 + np.arange(32, 64), 128 + np.arange(0, 32)])
    return np.ascontiguousarray(w[:, :, idx].reshape(3, 128, 4, 256).transpose(1, 0, 2, 3))


def prep_wukv(w):
    kn = w[:, :, :128].reshape(256, 512)
    vv = w[:, :, 128:].reshape(256, 512)
    return np.ascontiguousarray(np.concatenate([kn, vv], axis=1).reshape(2, 128, 1024).transpose(1, 0, 2))


def prep_wg(w):
    return np.ascontiguousarray(w.reshape(KC, 128, 3, KC, 128).transpose(3, 1, 0, 2, 4))


def prep_wb(w):
    return np.ascontiguousarray(w.reshape(3, 4, 128, KC, 128).transpose(3, 2, 0, 1, 4))


def prep_wout(w):
    return np.ascontiguousarray(w.reshape(KC, 128, D).transpose(1, 0, 2))


def colv(v, n):
    return np.ascontiguousarray(np.asarray(v, np.float32).reshape(n, 128).T)


def make_consts(inp, l, r):
    c = np.zeros((128, NCONST), np.float32)
    c[:, C_GF1:C_GF1 + 8] = colv(inp["ffn1_norm"][l], 8)
    c[:, C_GMIX:C_GMIX + 8] = colv(inp["mix_norm"][l], 8)
    c[:, C_GF2:C_GF2 + 8] = colv(inp["ffn2_norm"][l], 8)
    c[:, C_GQ:C_GQ + 3] = colv(inp["c_q_norm"][l], 3)
    c[:, C_GKV:C_GKV + 2] = colv(inp["c_kv_norm"][l], 2)
    c[:, C_BG:C_BG + 24] = colv(inp["b_gate"][l], 24)
    c[:, C_SUBLN] = inp["a_subln"][l]
    sk = inp["b_sinks"][l]
    for cc in range(4):
        c[0:64, C_SINK + cc] = sk[2 * cc]
        c[64:128, C_SINK + cc] = sk[2 * cc + 1]
    lam_init = 0.8 - 0.6 * math.exp(-0.3 * l)
    c[:, C_LAMI] = lam_init
    c[:, C_OML] = 1.0 - lam_init
    c[:, C_GFIN:C_GFIN + 8] = colv(inp["final_norm"], 8)
    c[:, C_ALAM:C_ALAM + 256] = inp["a_lambda"][l].reshape(1, 256)
    c[:, C_EPS] = EPS
    for s_ in range(4):
        c[:, C_HB + s_] = 0.0 if s_ == r - 1 else -30000.0
        c[:, C_HB + 4 + s_] = 0.0 if s_ == r + 1 else -30000.0
        for h in range(4):
            c[:, C_NFLAG + h * 4 + s_] = -8.0 * SLOPES_A[h] if s_ == r else 0.0
    return c


def rope_tables(r):
    pos = np.arange(r * T, (r + 1) * T, dtype=np.float32)
    inv_freq = (np.float32(10000.0) ** (-np.arange(0, 64, 2, dtype=np.float32) / np.float32(64))).astype(np.float32)
    ang = (pos[:, None] * inv_freq[None, :]).astype(np.float32)
    cos, sin = np.cos(ang).astype(np.float32).T, np.sin(ang).astype(np.float32).T
    cc = np.concatenate([cos, cos], axis=0)
    ss = np.concatenate([-sin, sin], axis=0)
    return np.ascontiguousarray(np.stack([cc, ss]).astype(np.float32))


def alibi_tables(r):
    qaug = np.zeros((32, T), np.float32)
    ql = np.arange(512)
    for qc in range(4):
        sl = slice(qc * 512, (qc + 1) * 512)
        qaug[4 * qc + 0, sl] = ql % 256
        qaug[4 * qc + 1, sl] = (ql // 256) * 256
        qaug[4 * qc + 2, sl] = 1.0
        qaug[4 * qc + 3, sl] = 1.0
    kaug = np.zeros((4, 32, SEQ), np.float32)
    kl = np.arange(128, dtype=np.float32)
    for h in range(4):
        s8 = 8.0 * SLOPES_A[h]
        for j in range(4):
            rank = j
            for blk in range(16):
                k0 = rank * T + blk * 128
                cols = slice(j * T + blk * 128, j * T + (blk + 1) * 128)
                for qc in range(4):
                    q0 = r * T + qc * 512
                    if k0 + 128 <= q0:
                        sg = 1.0
                    elif k0 >= q0 + 512:
                        sg = -1.0
                    else:
                        continue
                    kaug[h, 4 * qc + 0, cols] = -sg * s8
                    kaug[h, 4 * qc + 1, cols] = -sg * s8
                    kaug[h, 4 * qc + 2, cols] = sg * s8 * kl
                    kaug[h, 4 * qc + 3, cols] = -s8 * abs(q0 - k0)
    return kaug.astype(ml_dtypes.bfloat16), qaug.astype(ml_dtypes.bfloat16)


def window_tables():
    kl = np.arange(128)[:, None]
    j = np.arange(384)[None, :]
    d = np.abs(j - 128 - kl).astype(np.float32)
    bh = np.zeros((128, 8, 384), np.float32)
    for h in range(8):
        bh[:, h, :] = np.where(d <= 128, -8.0 * SLOPES_B[h] * d, -240000.0)
    jj = np.arange(896)[None, :]
    dw = np.abs(jj - kl - 384).astype(np.float32)
    return bh, np.ascontiguousarray(dw)


_PROG = []


def kernel(**inp):
    return _run(inp, DEPTH)


def _run(inp, depth):
    inp = {k: np.asarray(v) for k, v in inp.items()}
    x = inp["x"]
    cores = list(range(NCORES))
    L = range(DEPTH)
    shared = {
        "w13a": np.stack([prep_w13(inp["ffn1_w13"][l]) for l in L]),
        "w2a": np.stack([prep_w2(inp["ffn1_w2"][l]) for l in L]),
        "w13b": np.stack([prep_w13(inp["ffn2_w13"][l]) for l in L]),
        "w2b": np.stack([prep_w2(inp["ffn2_w2"][l]) for l in L]),
        "wuq": np.stack([prep_wuq(inp["c_w_uq"][l]) for l in L]),
        "wukv": np.stack([prep_wukv(inp["c_w_ukv"][l]) for l in L]),
        "wg": np.stack([prep_wg(inp["w_gate"][l]) for l in L]),
        "wb": np.stack([prep_wb(inp["w_branch"][l]) for l in L]),
        "wout": np.stack([prep_wout(inp["w_out"][l]) for l in L]),
    }
    wf = [prep_win(inp["w_in"][l]) for l in L]
    shared["winF"] = np.stack([a for a, _ in wf])
    shared["winT"] = np.stack([b for _, b in wf])
    shared["bh"], shared["dw"] = window_tables()
    ropes = [rope_tables(r) for r in range(4)]
    alibi = [alibi_tables(r) for r in range(4)]
    maps = []
    for c in cores:
        b, r = c // 4, c % 4
        m = dict(shared)
        m["xT_in"] = np.ascontiguousarray(x[b, r * T:(r + 1) * T, :].T)
        m["consts"] = np.stack([make_consts(inp, l, r) for l in L])
        m["rope"] = ropes[r]
        m["kaug"], m["qaug"] = alibi[r]
        maps.append(m)
    if not _PROG:
        _PROG.append(build_fused(depth))
    res = run_bass_kernel_spmd(_PROG[0], maps, core_ids=cores).results
    out = np.zeros((2, SEQ, D), np.float32)
    for c in cores:
        out[c // 4, (c % 4) * T:(c % 4 + 1) * T, :] = np.asarray(res[c]["yT"]).T
    return out
```
